# Optimizing a Trainium2 kernel written in Bass

```python
import math
import jax, jax.numpy as jnp
from jax import lax
import numpy as np

D_MODEL = 1024
BATCH = 32
SEQ = 2048
DEPTH = 1

PLE_DIM = 256
CONV_WIDTH = 1024
CONV_K = 3
N_HEADS = 8
HEAD_DIM = 64
V_DIM = 2 * HEAD_DIM
ATTN_WIDTH = N_HEADS * V_DIM
QK_WIDTH = N_HEADS * 2 * HEAD_DIM
Q_BLOCK = 128
LN_EPS = 1e-5
RMS_EPS = 1e-5

BRANCH_COLS = (CONV_WIDTH, CONV_WIDTH, CONV_WIDTH, CONV_WIDTH,
               QK_WIDTH, QK_WIDTH, ATTN_WIDTH, ATTN_WIDTH,
               D_MODEL, D_MODEL)
TOTAL_COLS = sum(BRANCH_COLS)
SPLIT_POINTS = tuple(int(s) for s in np.cumsum(BRANCH_COLS)[:-1])

kernel_name = "hybrid_shortconv_diffattn_deepnorm_encoder"


def alibi_slopes():
    return jnp.asarray(np.float32(2.0) ** (-8.0 * np.arange(1, N_HEADS + 1, dtype=np.float32) / N_HEADS))


def layer_norm(x, g, b):
    xf = x.astype(jnp.float32)
    mu = jnp.mean(xf, axis=-1, keepdims=True)
    xc = xf - mu
    var = jnp.mean(xc * xc, axis=-1, keepdims=True)
    y = xc * lax.rsqrt(var + LN_EPS) * g.astype(jnp.float32) + b.astype(jnp.float32)
    return y.astype(x.dtype)


def rms_norm(x, g):
    xf = x.astype(jnp.float32)
    y = xf * lax.rsqrt(jnp.mean(xf * xf, axis=-1, keepdims=True) + RMS_EPS) * g.astype(jnp.float32)
    return y.astype(x.dtype)


def short_conv_branch(u, c, bgate, z, conv_w, conv_b, w_proj):
    h = c * u
    h = lax.conv_general_dilated(
        h, conv_w[:, None, :], window_strides=(1,),
        padding=[(CONV_K // 2, CONV_K // 2)],
        dimension_numbers=("NWC", "WIO", "NWC"),
        feature_group_count=CONV_WIDTH) + conv_b
    y = bgate * h * jax.nn.silu(z)
    return y @ w_proj


def diff_attention_branch(q, k, v, z, lq1, lk1, lq2, lk2, subln_g, w_proj, slopes, lambda_init):
    bsz, seq, _ = q.shape
    q = q.reshape(bsz, seq, N_HEADS, 2, HEAD_DIM) * (HEAD_DIM ** -0.5)
    k = k.reshape(bsz, seq, N_HEADS, 2, HEAD_DIM)
    v = v.reshape(bsz, seq, N_HEADS, V_DIM)
    lam = (jnp.exp(jnp.sum(lq1.astype(jnp.float32) * lk1.astype(jnp.float32)))
           - jnp.exp(jnp.sum(lq2.astype(jnp.float32) * lk2.astype(jnp.float32)))
           + lambda_init)
    n_blocks = seq // Q_BLOCK
    q_blocks = q.reshape(bsz, n_blocks, Q_BLOCK, N_HEADS, 2, HEAD_DIM).transpose(1, 0, 2, 3, 4, 5)
    q_pos = jnp.arange(seq, dtype=jnp.int32).reshape(n_blocks, Q_BLOCK)
    k_pos = jnp.arange(seq, dtype=jnp.int32)

    def attend(args):
        qb, qp = args
        logits = jnp.einsum("bqhmd,bkhmd->bhmqk", qb, k).astype(jnp.float32)
        dist = jnp.abs(qp[:, None] - k_pos[None, :]).astype(jnp.float32)
        logits = logits - slopes[:, None, None, None] * dist
        probs = jax.nn.softmax(logits, axis=-1)
        weights = probs[:, :, 0] - lam * probs[:, :, 1]
        return jnp.einsum("bhqk,bkhe->bqhe", weights.astype(v.dtype), v)

    o = lax.map(attend, (q_blocks, q_pos))
    o = o.transpose(1, 0, 2, 3, 4).reshape(bsz, seq, N_HEADS, V_DIM)
    o = rms_norm(o, subln_g) * (1.0 - lambda_init)
    o = o.reshape(bsz, seq, ATTN_WIDTH) * jax.nn.silu(z)
    return o @ w_proj


def setup_inputs(seed: int = 0) -> dict:
    key = jax.random.key(seed)
    ks = jax.random.split(key, 17)
    beta = (8.0 * DEPTH) ** -0.25
    f32 = jnp.float32
    col_scale = jnp.concatenate([
        jnp.full((n,), s, dtype=f32) for n, s in zip(
            BRANCH_COLS, (beta, 1.0, 1.0, 1.0, 1.0, 1.0, beta, 1.0, 1.0, 1.0))])
    x = jax.random.normal(ks[0], (BATCH, SEQ, D_MODEL), f32)
    p = jax.random.normal(ks[1], (DEPTH, BATCH, SEQ, PLE_DIM), f32)
    w_in = jax.random.normal(ks[2], (DEPTH, D_MODEL, TOTAL_COLS), f32) * (D_MODEL ** -0.5) * col_scale
    conv_w = jax.random.normal(ks[3], (DEPTH, CONV_K, CONV_WIDTH), f32) * (CONV_K ** -0.5)
    conv_b = 0.01 * jax.random.normal(ks[4], (DEPTH, CONV_WIDTH), f32)
    w_proj_a = jax.random.normal(ks[5], (DEPTH, CONV_WIDTH, D_MODEL), f32) * (CONV_WIDTH ** -0.5) * beta
    lambda_q1 = 0.1 * jax.random.normal(ks[6], (DEPTH, HEAD_DIM), f32)
    lambda_k1 = 0.1 * jax.random.normal(ks[7], (DEPTH, HEAD_DIM), f32)
    lambda_q2 = 0.1 * jax.random.normal(ks[8], (DEPTH, HEAD_DIM), f32)
    lambda_k2 = 0.1 * jax.random.normal(ks[9], (DEPTH, HEAD_DIM), f32)
    subln_g = 1.0 + 0.01 * jax.random.normal(ks[10], (DEPTH, V_DIM), f32)
    w_proj_b = jax.random.normal(ks[11], (DEPTH, ATTN_WIDTH, D_MODEL), f32) * (ATTN_WIDTH ** -0.5) * beta
    w_out = jax.random.normal(ks[12], (DEPTH, D_MODEL, D_MODEL), f32) * (D_MODEL ** -0.5) * beta
    w_ple = jax.random.normal(ks[13], (DEPTH, PLE_DIM, D_MODEL), f32) * (PLE_DIM ** -0.5)
    w_ple_gate = jax.random.normal(ks[14], (DEPTH, D_MODEL, D_MODEL), f32) * (D_MODEL ** -0.5)
    ln_g = 1.0 + 0.01 * jax.random.normal(ks[15], (DEPTH, D_MODEL), f32)
    ln_b = 0.01 * jax.random.normal(ks[16], (DEPTH, D_MODEL), f32)
    return {"x": x, "p": p, "w_in": w_in, "conv_w": conv_w, "conv_b": conv_b,
            "w_proj_a": w_proj_a, "lambda_q1": lambda_q1, "lambda_k1": lambda_k1,
            "lambda_q2": lambda_q2, "lambda_k2": lambda_k2, "subln_g": subln_g,
            "w_proj_b": w_proj_b, "w_out": w_out, "w_ple": w_ple,
            "w_ple_gate": w_ple_gate, "ln_g": ln_g, "ln_b": ln_b}


def reference(x, p, w_in, conv_w, conv_b, w_proj_a, lambda_q1, lambda_k1, lambda_q2,
              lambda_k2, subln_g, w_proj_b, w_out, w_ple, w_ple_gate, ln_g, ln_b):
    alpha = (2.0 * DEPTH) ** 0.25
    slopes = alibi_slopes()
    h = x
    for i in range(DEPTH):
        lambda_init = 0.8 - 0.6 * math.exp(-0.3 * i)
        proj = h @ w_in[i]
        u, c, bg, za, q, k, v, zb, ga, gb = jnp.split(proj, SPLIT_POINTS, axis=-1)
        y_a = short_conv_branch(u, c, bg, za, conv_w[i], conv_b[i], w_proj_a[i])
        y_b = diff_attention_branch(q, k, v, zb, lambda_q1[i], lambda_k1[i], lambda_q2[i],
                                    lambda_k2[i], subln_g[i], w_proj_b[i], slopes, lambda_init)
        merged = jax.nn.sigmoid(ga) * y_a + jax.nn.sigmoid(gb) * y_b
        r = alpha * h + merged @ w_out[i]
        r = r + jax.nn.sigmoid(r @ w_ple_gate[i]) * (p[i] @ w_ple[i])
        h = layer_norm(r, ln_g[i], ln_b[i])
    return h
```

```python
import math
from contextlib import ExitStack

import numpy as np
import concourse.bass as bass
import concourse.mybir as mybir
from concourse.bass_utils import run_bass_kernel_spmd

F32 = mybir.dt.float32
BF16 = mybir.dt.bfloat16
AF = mybir.ActivationFunctionType
ALU = mybir.AluOpType
AX = mybir.AxisListType

D_MODEL = 1024
N_HEADS = 8
LN_EPS = 1e-5
RMS_EPS = 1e-5
ALPHA = 2.0 ** 0.25
LAMBDA_INIT = 0.8 - 0.6 * math.exp(0.0)
N_CORES = 8
NW = 3


def A(fn, *args, **kw):
    return (fn, args, kw)


class SemC:
    def __init__(self, sem, name):
        self.sem = sem
        self.name = name
        self.count = 0


class Eng:
    def __init__(self, name, h, semc, is_pe=False):
        self.name = name
        self.h = h
        self.semc = semc
        self.is_pe = is_pe
        self.waited = {}
        self.prog = []


class Buf:
    def __init__(self, name, excl=False):
        self.name = name
        self.excl = excl
        self.w = {}
        self.r = {}


class Prog:
    def __init__(self, nc):
        self.nc = nc
        self.stack = ExitStack()
        self.semcs = []
        self.pe = Eng("pe", nc.tensor, self.new_sem("c_pe"), is_pe=True)
        self.act = Eng("act", nc.scalar, self.new_sem("c_act"))
        self.dve = Eng("dve", nc.vector, self.new_sem("c_dve"))
        self.pool = Eng("pool", nc.gpsimd, self.new_sem("c_pool"))
        self.sp = Eng("sp", nc.sync, self.new_sem("c_sp"))
        self.engs = [self.pe, self.act, self.dve, self.pool, self.sp]

    def new_sem(self, name):
        sem = self.stack.enter_context(self.nc.semaphore(name))
        sc = SemC(sem, name)
        self.semcs.append(sc)
        return sc

    @staticmethod
    def _merge(d, src):
        for sc, v in src.items():
            if d.get(sc, 0) < v:
                d[sc] = v

    def _deps(self, eng, reads, writes):
        d = {}
        for b in reads:
            self._merge(d, b.w)
            if b.excl:
                self._merge(d, {sc: v for sc, v in b.r.items() if sc is not eng.semc})
        for b in writes:
            self._merge(d, b.w)
            self._merge(d, b.r)
        return d

    def _waits(self, eng, d):
        for sc, v in d.items():
            if eng.is_pe and sc is eng.semc:
                continue
            if eng.waited.get(sc, 0) >= v:
                continue
            eng.waited[sc] = v
            eng.prog.append(lambda h=eng.h, s=sc.sem, v=v: h.wait_ge(s, v))

    @staticmethod
    def _record(sc, v, reads, writes):
        for b in reads:
            if b.r.get(sc, 0) < v:
                b.r[sc] = v
        for b in writes:
            b.w = {sc: v}
            b.r = {}

    def op(self, eng, fn, reads=(), writes=()):
        self.group(eng, [fn], reads, writes)

    def group(self, eng, fns, reads=(), writes=()):
        self._waits(eng, self._deps(eng, reads, writes))
        sc = eng.semc
        sc.count += 1
        v = sc.count
        for (f, a, kw) in fns[:-1]:
            eng.prog.append(lambda f=f, a=a, kw=kw: f(*a, **kw))
        (f, a, kw) = fns[-1]
        eng.prog.append(lambda f=f, a=a, kw=kw, s=sc.sem: f(*a, **kw).then_inc(s, 1))
        self._record(sc, v, reads, writes)

    def dma(self, eng, pairs, semc, reads=(), writes=()):
        self._waits(eng, self._deps(eng, reads, writes))
        for (o, i) in pairs:
            semc.count += 16
            eng.prog.append(lambda h=eng.h, o=o, i=i, s=semc.sem: h.dma_start(out=o, in_=i).then_inc(s, 16))
        self._record(semc, semc.count, reads, writes)

    def frontier(self):
        return {sc: sc.count for sc in self.semcs if sc.count > 0}

    def fresh(self, name):
        b = Buf(name)
        b.w = self.frontier()
        return b


class Region:
    def __init__(self, handle, words):
        self.handle = handle
        self.words = words
        self.off = 0

    def reset(self):
        self.off = 0

    def take(self, free_shape, dt):
        n = int(np.prod(free_shape))
        words = n if dt == F32 else (n + 1) // 2
        assert self.off + words <= self.words, ("region overflow", self.off, words, self.words)
        v = self.handle[:, self.off:self.off + words]
        self.off += words
        if dt != F32:
            v = v.bitcast(dt)
        if len(free_shape) == 2:
            v = v.rearrange("p (a b) -> p a b", a=free_shape[0])
        return v


class _Stop(Exception):
    pass


BAND_T = 144.0


def _min_dist(qt, kc):
    q_lo, q_hi = 512 * qt, 512 * qt + 511
    k_lo, k_hi = 128 * kc, 128 * kc + 127
    if k_lo > q_hi:
        return k_lo - q_hi
    if k_hi < q_lo:
        return q_lo - k_hi
    return 0


def build(S, NSEQ, dbg_stop=None):
    def chk(name):
        if dbg_stop == name:
            raise _Stop()

    NT = S // 512
    NKC = S // 128
    nc = bass.Bass("TRN2", target_bir_lowering=False)

    def din(name, shape):
        return nc.dram_tensor(name, list(shape), F32, kind="ExternalInput").ap()

    xT = din("xT", [NSEQ, 1024, S])
    pT = din("pT", [NSEQ, 256, S])
    w_in = din("w_in", [1024, 10240])
    w_pa = din("w_pa", [1024, 1024])
    w_pb = din("w_pb", [1024, 1024])
    w_out = din("w_out", [1024, 1024])
    w_pg = din("w_pg", [1024, 1024])
    w_ple = din("w_ple", [256, 1024])
    d_cw = din("cw", [128, 8, 3])
    d_cb = din("cb", [128, 8])
    d_subg = din("subg", [128, 1])
    d_lng = din("lng", [128, 8])
    d_lnb = din("lnb", [128, 8])
    d_lamv = din("lamv", [128, 4, 64])
    d_qaug = din("c_qaug", [4, S])
    d_kaug = din("c_kaug", [4, S])
    d_kaugn = din("c_kaugn", [4, S])
    d_dtab = din("c_dtab", [128, 128])
    d_ident = din("c_ident", [128, 128])
    outT = nc.dram_tensor("outT", [NSEQ, 1024, S], F32, kind="ExternalOutput").ap()

    pr = Prog(nc)
    st = pr.stack
    pe, act, dve, pool, sp = pr.pe, pr.act, pr.dve, pr.pool, pr.sp

    def sb(name, shape, dt):
        return st.enter_context(nc.sbuf_tensor(name, list(shape), dt))

    xbf = sb("xbf", [128, 8, S], BF16)
    ybuf = sb("ybuf", [128, 8, S], BF16)
    mabuf = sb("mabuf", [128, 8, S], BF16)
    wbuf = sb("wbuf", [128, NW, 8, 512], BF16)
    REGW = max(8 * S + 4 + 1024, 7 * S + 4608, 18432) + 64
    reg_t = sb("reg", [128, REGW], F32)
    region = Region(reg_t, REGW)
    ones_bf = sb("ones_bf", [128, 128], BF16)
    ones_f = sb("ones_f", [128, 128], F32)
    ident = sb("ident", [128, 128], BF16)
    dtab = sb("dtab", [128, 128], BF16)
    cw_t = sb("cw_t", [128, 8, 3], F32)
    cb_t = sb("cb_t", [128, 8], F32)
    subg_t = sb("subg_t", [128, 1], F32)
    lng_t = sb("lng_t", [128, 8], F32)
    lnb_t = sb("lnb_t", [128, 8], F32)
    lamv_t = sb("lamv_t", [128, 4, 64], F32)
    wple_t = sb("wple_t", [128, 2, 1024], BF16)
    sgt = [sb(f"sgt{i}", [128, 512], F32) for i in range(2)]
    _tgt0 = sb("tgt0", [128, 512], F32)
    tgt = [_tgt0, _tgt0]
    b_sg = [Buf("sg0"), Buf("sg1")]
    _b_tg0 = Buf("tg0")
    b_tg = [_b_tg0, _b_tg0]
    sm = sb("small", [128, 16], F32)
    ps = st.enter_context(nc.psum_tensor("ps", [128, 8, 512], F32))

    banks = [Buf(f"bank{i}", excl=True) for i in range(8)]
    b_const = Buf("const")
    b_xbf = Buf("xbf")
    b_y = [Buf(f"y{i}") for i in range(8)]
    b_ma = [Buf(f"ma{i}") for i in range(8)]
    b_w = [Buf(f"w{i}") for i in range(NW)]
    s_w = [pr.new_sem(f"d_w{i}") for i in range(NW)]
    s_setup = pr.new_sem("d_setup")
    s_setup2 = pr.new_sem("d_setup2")
    b_const2 = Buf("const2")
    s_xbf = pr.new_sem("d_xbf")
    s_aug = pr.new_sem("d_aug")
    s_xres = [pr.new_sem("d_xres0"), pr.new_sem("d_xres1")]
    s_ptb = pr.new_sem("d_ptb")
    s_out = pr.new_sem("d_out")

    ring = {"i": 0}

    def next_bank(lo=0, hi=8):
        n = hi - lo
        i = lo + (ring["i"] % n)
        ring["i"] += 1
        return i

    def bank_ap(i, rows=slice(0, 128), cols=slice(0, 512)):
        return ps[rows, i, cols]

    def wv(src, c0, n):
        return src[:, c0:c0 + n].rearrange("(c p) n -> p c n", p=128)

    plan = []
    for s in range(NSEQ):
        for cc in range(8):
            plan.append((("conv", s, cc),
                         [(r * 128, 128, 8, wv(w_in, r * 1024 + cc * 128, 128)) for r in range(4)]))
        for oc in range(8):
            plan.append((("pa", s, oc),
                         [(0, 128, 8, wv(w_in, 8192 + oc * 128, 128)),
                          (128, 128, 8, wv(w_pa, oc * 128, 128))]))
        for h in range(8):
            if h % 4 == 0:
                plan.append((("v", s, h // 4), [(0, 512, 8, wv(w_in, 6144 + (h // 4) * 512, 512))]))
            plan.append((("qkz", s, h),
                         [(0, 128, 8, wv(w_in, 4096 + h * 128, 128)),
                          (128, 128, 8, wv(w_in, 5120 + h * 128, 128)),
                          (256, 128, 8, wv(w_in, 7168 + h * 128, 128))]))
        for oc in range(8):
            plan.append((("pb", s, oc),
                         [(0, 128, 8, wv(w_in, 9216 + oc * 128, 128)),
                          (128, 128, 8, wv(w_pb, oc * 128, 128))]))
        for tt in range(NT):
            for g in range(2):
                plan.append((("wo", s, tt, g), [(0, 512, 8, wv(w_out, g * 512, 512))]))
            for g in range(2):
                plan.append((("pg", s, tt, g), [(0, 512, 8, wv(w_pg, g * 512, 512))]))

    wstate = {"issued": 0, "next": 0}

    def w_issue(i):
        key, segs = plan[i]
        slot = i % NW
        pairs = []
        for (c0, n, nk, src) in segs:
            if nk == 8:
                pairs.append((wbuf[:, slot, :, c0:c0 + n], src))
            elif nk == 2:
                pairs.append((wbuf[:, slot, 0:2, c0:c0 + n], src))
            else:
                pairs.append((wbuf[:, slot, 2:4, c0:c0 + n], src))
        pr.dma(pool, pairs, s_w[slot], reads=(), writes=(b_w[slot],))

    def w_next(key):
        i = wstate["next"]
        assert plan[i][0] == key, (plan[i][0], key)
        while wstate["issued"] < min(len(plan), i + NW):
            w_issue(wstate["issued"])
            wstate["issued"] += 1
        wstate["next"] += 1
        slot = i % NW
        return (lambda kc, c0, n: wbuf[:, slot, kc, c0:c0 + n]), b_w[slot]

    def mm_group(bank_i, lhs_fn, rhs_fn, nk, reads, rows=slice(0, 128), cols=slice(0, 512)):
        fns = []
        for kc in range(nk):
            fns.append(A(nc.tensor.matmul, bank_ap(bank_i, rows, cols), lhs_fn(kc), rhs_fn(kc),
                                                      start=(kc == 0), stop=(kc == nk - 1)))
        pr.group(pe, fns, reads=reads, writes=(banks[bank_i],))

    pr.dma(sp, [(cw_t[:], d_cw), (cb_t[:], d_cb), (subg_t[:], d_subg), (lng_t[:], d_lng),
                (lnb_t[:], d_lnb), (lamv_t[:], d_lamv)], s_setup, writes=(b_const,))
    pr.dma(pool, [(ident[:], d_ident), (dtab[:], d_dtab),
                  (wple_t[:, :, :], w_ple.rearrange("(c p) n -> p c n", p=128))], s_setup2, writes=(b_const2,))
    pr.op(dve, A(nc.vector.memset, ones_bf[:], 1.0), writes=(b_const,))
    pr.op(dve, A(nc.vector.memset, ones_f[:], 1.0), writes=(b_const,))
    b_sm = Buf("sm")
    lprod = sb("lprod", [128, 2, 64], F32)
    pr.op(dve, A(nc.vector.tensor_tensor, out=lprod[:, 0, :], in0=lamv_t[:, 0, :], in1=lamv_t[:, 1, :], op=ALU.mult),
          reads=(b_const,), writes=(b_sm,))
    pr.op(dve, A(nc.vector.tensor_tensor, out=lprod[:, 1, :], in0=lamv_t[:, 2, :], in1=lamv_t[:, 3, :], op=ALU.mult),
          reads=(b_const,), writes=(b_sm,))
    pr.op(dve, A(nc.vector.tensor_reduce, out=sm[:, 0:1], in_=lprod[:, 0, :], axis=AX.X, op=ALU.add),
          reads=(b_sm,), writes=(b_sm,))
    pr.op(dve, A(nc.vector.tensor_reduce, out=sm[:, 1:2], in_=lprod[:, 1, :], axis=AX.X, op=ALU.add),
          reads=(b_sm,), writes=(b_sm,))
    pr.op(act, A(nc.scalar.activation, out=sm[:, 2:4], in_=sm[:, 0:2], func=AF.Exp), reads=(b_sm,), writes=(b_sm,))
    pr.op(dve, A(nc.vector.tensor_tensor, out=sm[:, 4:5], in0=sm[:, 2:3], in1=sm[:, 3:4], op=ALU.subtract),
          reads=(b_sm,), writes=(b_sm,))
    pr.op(dve, A(nc.vector.tensor_scalar, out=sm[:, 5:6], in0=sm[:, 4:5], scalar1=float(LAMBDA_INIT), scalar2=-1.0,
                                               op0=ALU.add, op1=ALU.mult), reads=(b_sm,), writes=(b_sm,))
    pr.op(dve, A(nc.vector.tensor_scalar, out=sm[:, 6:7], in0=subg_t[:, 0:1],
                                               scalar1=float((1.0 - LAMBDA_INIT) * math.sqrt(128.0)), scalar2=0.0,
                                               op0=ALU.mult, op1=ALU.add), reads=(b_const, b_sm), writes=(b_sm,))
    pr.op(dve, A(nc.vector.memset, sm[:, 7:8], float(128.0 * RMS_EPS)), writes=(b_sm,))
    pr.op(dve, A(nc.vector.memset, sm[:, 8:9], float(LN_EPS)), writes=(b_sm,))
    neglam = sm[:, 5:6]
    geff = sm[:, 6:7]
    eps_rms = sm[:, 7:8]
    eps_ln = sm[:, 8:9]

    def load_x(s, kcs_=range(8)):
        xv = xT[s].rearrange("(c p) n -> p c n", p=128)
        pr.dma(pool, [(xbf[:, kc, :], xv[:, kc, :]) for kc in kcs_], s_xbf, writes=(b_xbf,))

    load_x(0)

    def run_seq(s):
        region.reset()
        hb = [region.take((S + 2,), F32) for _ in range(2)]
        bgb = [region.take((S,), F32) for _ in range(2)]
        zsb_c = [region.take((S,), F32) for _ in range(2)]
        tb = [region.take((S,), F32) for _ in range(2)]
        usb = [region.take((512,), F32) for _ in range(2)]
        b_h = [pr.fresh("h0"), pr.fresh("h1")]
        b_bg = [pr.fresh("bg0"), pr.fresh("bg1")]
        b_zs = [pr.fresh("zs0"), pr.fresh("zs1")]
        b_t = [pr.fresh("t0"), pr.fresh("t1")]
        b_u = [pr.fresh("u0"), pr.fresh("u1")]
        for k in range(2):
            pr.op(dve, A(nc.vector.memset, hb[k][:, 0:1], 0.0), writes=(b_h[k],))
            pr.op(dve, A(nc.vector.memset, hb[k][:, S + 1:S + 2], 0.0), writes=(b_h[k],))
        ucount = 0
        for cc in range(8):
            k = cc % 2
            wf, wb = w_next(("conv", s, cc))
            for tt in range(NT):
                tsl = slice(tt * 512, (tt + 1) * 512)
                bi = [next_bank() for _ in range(4)]
                for j in range(4):
                    mm_group(bi[j], lambda kc, j=j: wf(kc, j * 128, 128), lambda kc: xbf[:, kc, tsl], 8,
                             reads=(wb, b_xbf))
                uk = ucount % 2
                ucount += 1
                pr.op(act, A(nc.scalar.activation, out=usb[uk][:], in_=bank_ap(bi[0]), func=AF.Identity),
                      reads=(banks[bi[0]],), writes=(b_u[uk],))
                pr.op(dve, A(nc.vector.tensor_tensor,
                    out=hb[k][:, 1 + tt * 512:1 + (tt + 1) * 512], in0=bank_ap(bi[1]), in1=usb[uk][:], op=ALU.mult),
                    reads=(banks[bi[1]], b_u[uk]), writes=(b_h[k],))
                pr.op(act, A(nc.scalar.activation, out=bgb[k][:, tsl], in_=bank_ap(bi[2]), func=AF.Identity),
                      reads=(banks[bi[2]],), writes=(b_bg[k],))
                pr.op(act, A(nc.scalar.activation, out=zsb_c[k][:, tsl], in_=bank_ap(bi[3]), func=AF.Silu),
                      reads=(banks[bi[3]],), writes=(b_zs[k],))
            pr.op(act, A(nc.scalar.activation, out=tb[k][:], in_=hb[k][:, 1:S + 1], func=AF.Identity,
                                                               scale=cw_t[:, cc, 1:2], bias=cb_t[:, cc:cc + 1]),
                  reads=(b_h[k], b_const), writes=(b_t[k],))
            pr.op(dve, A(nc.vector.scalar_tensor_tensor, out=tb[k][:], in0=hb[k][:, 0:S], scalar=cw_t[:, cc, 0:1],
                                                                        in1=tb[k][:], op0=ALU.mult, op1=ALU.add),
                  reads=(b_h[k], b_const, b_t[k]), writes=(b_t[k],))
            pr.op(dve, A(nc.vector.scalar_tensor_tensor, out=tb[k][:], in0=hb[k][:, 2:S + 2], scalar=cw_t[:, cc, 2:3],
                                                                        in1=tb[k][:], op0=ALU.mult, op1=ALU.add),
                  reads=(b_h[k], b_const, b_t[k]), writes=(b_t[k],))
            pr.op(dve, A(nc.vector.tensor_tensor, out=tb[k][:], in0=tb[k][:], in1=bgb[k][:], op=ALU.mult),
                  reads=(b_t[k], b_bg[k]), writes=(b_t[k],))
            pr.op(dve, A(nc.vector.tensor_tensor, out=ybuf[:, cc, :], in0=tb[k][:], in1=zsb_c[k][:], op=ALU.mult),
                  reads=(b_t[k], b_zs[k]), writes=(b_y[cc],))

        chk("conv")
        gcount = 0
        for oc in range(8):
            wf, wb = w_next(("pa", s, oc))
            for tt in range(NT):
                tsl = slice(tt * 512, (tt + 1) * 512)
                bg_i = next_bank()
                by_i = next_bank()
                mm_group(bg_i, lambda kc: wf(kc, 0, 128), lambda kc: xbf[:, kc, tsl], 8, reads=(wb, b_xbf))
                mm_group(by_i, lambda kc: wf(kc, 128, 128), lambda kc: ybuf[:, kc, tsl], 8, reads=(wb,) + tuple(b_y))
                k = gcount % 2
                gcount += 1
                pr.op(act, A(nc.scalar.activation, out=sgt[k][:], in_=bank_ap(bg_i), func=AF.Sigmoid),
                      reads=(banks[bg_i],), writes=(b_sg[k],))
                pr.op(dve, A(nc.vector.tensor_tensor,
                    out=mabuf[:, oc, tsl], in0=bank_ap(by_i), in1=sgt[k][:], op=ALU.mult),
                    reads=(banks[by_i], b_sg[k]), writes=(b_ma[oc],))

        chk("pa")
        region.reset()
        QA = region.take((S,), BF16)
        QB = region.take((S,), BF16)
        KAp = region.take((S,), BF16)
        KAn = region.take((S,), BF16)
        KBp = region.take((S,), BF16)
        KBn = region.take((S,), BF16)
        V4 = region.take((NKC, 512), BF16)
        zsb2 = [region.take((S,), F32) for _ in range(2)]
        Et = [region.take((2, 512), BF16) for _ in range(3)]
        r_t = [region.take((512,), F32) for _ in range(2)]
        o_t = [region.take((512,), F32) for _ in range(2)]
        sq_t = region.take((512,), F32)
        rs_t = region.take((512,), F32)
        b_QA, b_QB = pr.fresh("QA"), pr.fresh("QB")
        b_KAp, b_KAn, b_KBp, b_KBn = pr.fresh("KAp"), pr.fresh("KAn"), pr.fresh("KBp"), pr.fresh("KBn")
        b_V4 = pr.fresh("V4")
        b_zsb2 = [pr.fresh("zsb0"), pr.fresh("zsb1")]
        b_E = [pr.fresh(f"E{i}") for i in range(3)]
        b_r = [pr.fresh("r0"), pr.fresh("r1")]
        b_o = [pr.fresh("o0"), pr.fresh("o1")]
        b_sq = pr.fresh("sq")
        b_rs = pr.fresh("rs")
        for (t_, b_) in ((QB, b_QB), (KBp, b_KBp), (KBn, b_KBn)):
            pr.op(dve, A(nc.vector.memset, t_[0:64, :], 0.0), writes=(b_,))
        pr.dma(pool, [(QA[64:68, :], d_qaug), (QB[0:4, :], d_qaug), (KAp[64:68, :], d_kaug), (KAn[64:68, :], d_kaugn),
                      (KBp[0:4, :], d_kaug), (KBn[0:4, :], d_kaugn)], s_aug,
               writes=(b_QA, b_QB, b_KAp, b_KAn, b_KBp, b_KBn))
        chk("attn_setup")
        SB = lambda m, par: 2 * par + m
        OB = lambda m: 4 + m
        ZB = lambda m: 6 + m
        pending = [None, None]
        for h in range(8):
            zsb, b_zsb = zsb2[h % 2], b_zsb2[h % 2]
            mh = 2.0 ** (-(h + 1))
            qscale = 0.125 / mh
            hh = h % 4
            if hh == 0:
                wfv, wbv = w_next(("v", s, h // 4))
                for tc in range(NKC):
                    bi = next_bank(0, 4)
                    mm_group(bi, lambda kc, tc=tc: xbf[:, kc, tc * 128:(tc + 1) * 128], lambda kc: wfv(kc, 0, 512), 8,
                             reads=(wbv, b_xbf))
                    if tc % 2 == 0:
                        pr.op(act, A(nc.scalar.activation, out=V4[:, tc, :], in_=bank_ap(bi), func=AF.Identity),
                              reads=(banks[bi],), writes=(b_V4,))
                    else:
                        pr.op(dve, A(nc.vector.tensor_copy, out=V4[:, tc, :], in_=bank_ap(bi)),
                              reads=(banks[bi],), writes=(b_V4,))
            chk("attn_v")
            wf, wb = w_next(("qkz", s, h))
            lo, hi = slice(0, 64), slice(64, 128)
            for tt in range(NT):
                tsl = slice(tt * 512, (tt + 1) * 512)
                bq = next_bank(0, 4)
                mm_group(bq, lambda kc: wf(kc, 0, 128), lambda kc: xbf[:, kc, tsl], 8, reads=(wb, b_xbf))
                pr.op(act, A(nc.scalar.activation, out=QA[lo, tsl], in_=bank_ap(bq, lo), func=AF.Identity,
                                                                      scale=float(qscale)),
                      reads=(banks[bq],), writes=(b_QA,))
                pr.op(dve, A(nc.vector.tensor_scalar, out=QB[hi, tsl], in0=bank_ap(bq, hi), scalar1=float(qscale),
                                                                        scalar2=0.0, op0=ALU.mult, op1=ALU.add),
                      reads=(banks[bq],), writes=(b_QB,))
                chk("attn_q")
                bk = next_bank(0, 4)
                mm_group(bk, lambda kc: wf(kc, 128, 128), lambda kc: xbf[:, kc, tsl], 8, reads=(wb, b_xbf))
                pr.op(act, A(nc.scalar.activation, out=KAp[lo, tsl], in_=bank_ap(bk, lo), func=AF.Identity),
                      reads=(banks[bk],), writes=(b_KAp,))
                pr.op(dve, A(nc.vector.tensor_copy, out=KAn[lo, tsl], in_=bank_ap(bk, lo)),
                      reads=(banks[bk],), writes=(b_KAn,))
                pr.op(act, A(nc.scalar.activation, out=KBp[hi, tsl], in_=bank_ap(bk, hi), func=AF.Identity),
                      reads=(banks[bk],), writes=(b_KBp,))
                pr.op(dve, A(nc.vector.tensor_copy, out=KBn[hi, tsl], in_=bank_ap(bk, hi)),
                      reads=(banks[bk],), writes=(b_KBn,))
                chk("attn_k")
                bz = next_bank(0, 4)
                mm_group(bz, lambda kc: wf(kc, 256, 128), lambda kc: xbf[:, kc, tsl], 8, reads=(wb, b_xbf))
                pr.op(act, A(nc.scalar.activation, out=zsb[:, tsl], in_=bank_ap(bz), func=AF.Silu),
                      reads=(banks[bz],), writes=(b_zsb,))

            chk("attn_proj")
            maps = [
                (QA[0:68, :], KAp[0:68, :], KAn[0:68, :], QA[0:64, :], KAp[0:64, :], (b_QA, b_KAp, b_KAn)),
                (QB[0:128, :], KBp[0:128, :], KBn[0:128, :], QB[64:128, :], KBp[64:128, :], (b_QB, b_KBp, b_KBn)),
            ]

            for qt in range(NT):
                q0 = qt * 512
                kcs = [kc for kc in range(NKC) if mh * _min_dist(qt, kc) < BAND_T]
                assert len(kcs) >= 4

                def qk(kc, q0=q0, qt=qt):
                    k0 = kc * 128
                    for m in range(2):
                        Qa, Kp, Kn, Qd, Kd, bufs = maps[m]
                        bi = SB(m, kc % 2)
                        fns = []
                        if kc < 4 * qt:
                            fns.append(A(nc.tensor.matmul,
                                bank_ap(bi), Kp[:, k0:k0 + 128], Qa[:, q0:q0 + 512], start=True, stop=True))
                        elif kc >= 4 * qt + 4:
                            fns.append(A(nc.tensor.matmul,
                                bank_ap(bi), Kn[:, k0:k0 + 128], Qa[:, q0:q0 + 512], start=True, stop=True))
                        else:
                            j = kc - 4 * qt
                            if j > 0:
                                fns.append(A(nc.tensor.matmul,
                                    bank_ap(bi, cols=slice(0, 128 * j)), Kn[:, k0:k0 + 128], Qa[:, q0:q0 + 128 * j],
                                    start=True, stop=True))
                            fns.append(A(nc.tensor.matmul,
                                bank_ap(bi, cols=slice(128 * j, 512)), Kp[:, k0:k0 + 128],
                                Qa[:, q0 + 128 * j:q0 + 512], start=True, stop=False, skip_group_check=True))
                            fns.append(A(nc.tensor.matmul,
                                bank_ap(bi, cols=slice(128 * j, 128 * j + 128)), ident[:], dtab[:], start=False, stop=True,
                                skip_group_check=True))
                        pr.group(pe, fns, reads=bufs + (b_const, b_const2), writes=(banks[bi],))
                    e = kc % 3
                    par = kc % 2
                    pr.op(act, A(nc.scalar.activation, out=Et[e][:, :, :], in_=ps[:, 2 * par:2 * par + 2, :], func=AF.Exp,
                                 scale=float(mh)),
                          reads=(banks[SB(0, par)], banks[SB(1, par)]), writes=(b_E[e],))

                def pv(kc):
                    e = kc % 3
                    for m in range(2):
                        fns = [
                            A(nc.tensor.matmul, bank_ap(OB(m)), V4[:, kc, hh * 128:(hh + 1) * 128], Et[e][:, m, :],
                                                              start=(kc == kcs[0]), stop=(kc == kcs[-1])),
                            A(nc.tensor.matmul, bank_ap(ZB(m)), ones_bf[:], Et[e][:, m, :],
                                                              start=(kc == kcs[0]), stop=(kc == kcs[-1])),
                        ]
                        pr.group(pe, fns, reads=(b_E[e], b_V4, b_const), writes=(banks[OB(m)], banks[ZB(m)]))

                for i, kc in enumerate(kcs):
                    qk(kc)
                    chk("attn_qk")
                    if i >= 1:
                        pv(kcs[i - 1])
                        chk("attn_pv")
                    if i == 1 and pending[0] is not None:
                        pending[0]()
                        pending[0] = None
                    if i == 3 and pending[1] is not None:
                        pending[1](par=(kcs[4] % 2 if len(kcs) > 4 else 0))
                        pending[1] = None
                pv(kcs[-1])
                chk("attn_tile")
                for m in range(2):
                    pr.op(act, A(nc.scalar.activation, out=r_t[m][:], in_=bank_ap(ZB(m)), func=AF.Ln),
                          reads=(banks[ZB(m)],), writes=(b_r[m],))
                    pr.op(dve, A(nc.vector.tensor_copy, out=o_t[m][:], in_=bank_ap(OB(m))),
                          reads=(banks[OB(m)],), writes=(b_o[m],))

                def postB():
                    for m in range(2):
                        pr.op(act, A(nc.scalar.activation, out=r_t[m][:], in_=r_t[m][:], func=AF.Exp, scale=-1.0),
                              reads=(b_r[m],), writes=(b_r[m],))
                    for m in range(2):
                        pr.op(dve, A(nc.vector.tensor_tensor, out=o_t[m][:], in0=o_t[m][:], in1=r_t[m][:], op=ALU.mult),
                              reads=(b_o[m], b_r[m]), writes=(b_o[m],))
                    pr.op(dve, A(nc.vector.scalar_tensor_tensor, out=o_t[0][:], in0=o_t[1][:], scalar=neglam, in1=o_t[0][:],
                                 op0=ALU.mult, op1=ALU.add),
                          reads=(b_o[0], b_o[1], b_sm), writes=(b_o[0],))
                    pr.op(dve, A(nc.vector.tensor_tensor, out=sq_t[:], in0=o_t[0][:], in1=o_t[0][:], op=ALU.mult),
                          reads=(b_o[0],), writes=(b_sq,))

                def postC(h=h, qt=qt, par=0, zsb=zsb, b_zsb=b_zsb):
                    bi = SB(0, par)
                    pr.group(pe, [A(nc.tensor.matmul, bank_ap(bi), ones_f[:], sq_t[:], start=True, stop=True)],
                             reads=(b_sq, b_const), writes=(banks[bi],))
                    pr.op(act, A(nc.scalar.activation, out=rs_t[:], in_=bank_ap(bi), func=AF.Ln, bias=eps_rms, scale=1.0),
                          reads=(banks[bi], b_sm), writes=(b_rs,))
                    pr.op(act, A(nc.scalar.activation, out=rs_t[:], in_=rs_t[:], func=AF.Exp, scale=-0.5),
                          reads=(b_rs,), writes=(b_rs,))
                    pr.op(dve, A(nc.vector.tensor_tensor, out=o_t[0][:], in0=o_t[0][:], in1=rs_t[:], op=ALU.mult),
                          reads=(b_o[0], b_rs), writes=(b_o[0],))
                    pr.op(dve, A(nc.vector.scalar_tensor_tensor,
                                 out=ybuf[:, h, qt * 512:(qt + 1) * 512], in0=o_t[0][:], scalar=geff,
                                 in1=zsb[:, qt * 512:(qt + 1) * 512], op0=ALU.mult, op1=ALU.mult),
                          reads=(b_o[0], b_zsb, b_sm), writes=(b_y[h],))

                chk("attn_post1")
                if qt == NT - 1 and h == 7:
                    postB()
                    postC()
                    chk("attn_post2")
                else:
                    pending[0] = postB
                    pending[1] = postC

        chk("attn")
        def phase_pb():
            gcount = 0
            for oc in range(8):
                wf, wb = w_next(("pb", s, oc))
                for tt in range(NT):
                    tsl = slice(tt * 512, (tt + 1) * 512)
                    bg_i = next_bank()
                    by_i = next_bank()
                    mm_group(bg_i, lambda kc: wf(kc, 0, 128), lambda kc: xbf[:, kc, tsl], 8, reads=(wb, b_xbf))
                    mm_group(by_i, lambda kc: wf(kc, 128, 128), lambda kc: ybuf[:, kc, tsl], 8, reads=(wb,) + tuple(b_y))
                    k = gcount % 2
                    gcount += 1
                    pr.op(act, A(nc.scalar.activation, out=sgt[k][:], in_=bank_ap(bg_i), func=AF.Sigmoid),
                          reads=(banks[bg_i],), writes=(b_sg[k],))
                    pr.op(dve, A(nc.vector.tensor_tensor, out=tgt[k][:], in0=bank_ap(by_i), in1=sgt[k][:], op=ALU.mult),
                          reads=(banks[by_i], b_sg[k]), writes=(b_tg[k],))
                    pr.op(dve, A(nc.vector.tensor_tensor, out=mabuf[:, oc, tsl], in0=tgt[k][:],
                                                                                   in1=mabuf[:, oc, tsl], op=ALU.add),
                          reads=(b_tg[k], b_ma[oc]), writes=(b_ma[oc],))


        region.reset()
        X = [region.take((8, 512), F32) for _ in range(2)]
        sqq = region.take((8, 512), F32)
        rbf = region.take((8, 512), BF16)
        ptb = region.take((2, 512), BF16)
        gs_t = [region.take((512,), F32) for _ in range(2)]
        tg_t = [region.take((512,), F32) for _ in range(2)]
        mean_t = region.take((512,), F32)
        msq_t = region.take((512,), F32)
        rstd_t = region.take((512,), F32)
        b_ptb = pr.fresh("ptb")
        b_X = [[pr.fresh(f"x{j}_{i}") for i in range(8)] for j in range(2)]
        b_sqq = [pr.fresh(f"sqq{i}") for i in range(8)]
        b_rbf = [pr.fresh(f"rbf{i}") for i in range(8)]
        b_gs = [pr.fresh("gs0"), pr.fresh("gs1")]
        b_tg2 = [pr.fresh("tg0"), pr.fresh("tg1")]
        b_mean, b_msq, b_rstd = pr.fresh("mean"), pr.fresh("msq"), pr.fresh("rstd")

        def load_res(tt):
            tsl = slice(tt * 512, (tt + 1) * 512)
            j = tt % 2
            pr.dma(sp, [(X[j][:, :, :], xT[s].rearrange("(c p) n -> p c n", p=128)[:, :, tsl])], s_xres[j],
                   writes=tuple(b_X[j]))

        def load_p(tt):
            tsl = slice(tt * 512, (tt + 1) * 512)
            pr.dma(pool, [(ptb[:, :, :], pT[s].rearrange("(c p) n -> p c n", p=128)[:, :, tsl])], s_ptb, writes=(b_ptb,))

        def stage_wo(tt):
            tsl = slice(tt * 512, (tt + 1) * 512)
            j = tt % 2
            rr = X[j]
            for g in range(2):
                wf, wb = w_next(("wo", s, tt, g))
                for oc in range(4 * g, 4 * g + 4):
                    bi = next_bank()
                    mm_group(bi, lambda kc, oc=oc: wf(kc, (oc % 4) * 128, 128), lambda kc: mabuf[:, kc, tsl], 8,
                             reads=(wb,) + tuple(b_ma))
                    pr.op(dve, A(nc.vector.scalar_tensor_tensor,
                                 out=rr[:, oc, :], in0=rr[:, oc, :], scalar=float(ALPHA), in1=bank_ap(bi),
                                 op0=ALU.mult, op1=ALU.add),
                          reads=(banks[bi], b_X[j][oc]), writes=(b_X[j][oc],))
                    pr.op(act, A(nc.scalar.activation, out=rbf[:, oc, :], in_=rr[:, oc, :], func=AF.Identity),
                          reads=(b_X[j][oc],), writes=(b_rbf[oc],))

        def stage_gate(tt):
            j = tt % 2
            rr = X[j]
            gcount = 0
            for g in range(2):
                wf, wb = w_next(("pg", s, tt, g))
                for oc in range(4 * g, 4 * g + 4):
                    bg_i = next_bank()
                    bp_i = next_bank()
                    mm_group(bg_i, lambda kc, oc=oc: wf(kc, (oc % 4) * 128, 128), lambda kc: rbf[:, kc, :], 8,
                             reads=(wb,) + tuple(b_rbf))
                    mm_group(bp_i, lambda kc, oc=oc: wple_t[:, kc, oc * 128:(oc + 1) * 128], lambda kc: ptb[:, kc, :], 2,
                             reads=(b_const2, b_ptb))
                    k = gcount % 2
                    gcount += 1
                    pr.op(act, A(nc.scalar.activation, out=gs_t[k][:], in_=bank_ap(bg_i), func=AF.Sigmoid),
                          reads=(banks[bg_i],), writes=(b_gs[k],))
                    pr.op(dve, A(nc.vector.tensor_tensor, out=tg_t[k][:], in0=bank_ap(bp_i), in1=gs_t[k][:], op=ALU.mult),
                          reads=(banks[bp_i], b_gs[k]), writes=(b_tg2[k],))
                    pr.op(dve, A(nc.vector.tensor_tensor, out=rr[:, oc, :], in0=rr[:, oc, :], in1=tg_t[k][:], op=ALU.add),
                          reads=(b_tg2[k], b_X[j][oc]), writes=(b_X[j][oc],))
            if tt + 1 < NT:
                load_p(tt + 1)

        def stage_stats(tt):
            j = tt % 2
            rr = X[j]
            bsum = next_bank()
            mm_group(bsum, lambda kc: ones_f[:], lambda kc: rr[:, kc, :], 8, reads=tuple(b_X[j]) + (b_const,))
            pr.op(act, A(nc.scalar.activation, out=sqq[:, :, :], in_=rr[:, :, :], func=AF.Square),
                  reads=tuple(b_X[j]), writes=tuple(b_sqq))
            bsq = next_bank()
            mm_group(bsq, lambda kc: ones_f[:], lambda kc: sqq[:, kc, :], 8, reads=tuple(b_sqq) + (b_const,))
            pr.op(dve, A(nc.vector.tensor_scalar, out=mean_t[:], in0=bank_ap(bsum), scalar1=1.0 / 1024.0, scalar2=0.0,
                         op0=ALU.mult, op1=ALU.add),
                  reads=(banks[bsum],), writes=(b_mean,))
            pr.op(dve, A(nc.vector.tensor_tensor, out=msq_t[:], in0=mean_t[:], in1=mean_t[:], op=ALU.mult),
                  reads=(b_mean,), writes=(b_msq,))
            pr.op(dve, A(nc.vector.scalar_tensor_tensor, out=rstd_t[:], in0=bank_ap(bsq), scalar=1.0 / 1024.0, in1=msq_t[:],
                         op0=ALU.mult, op1=ALU.subtract),
                  reads=(banks[bsq], b_msq), writes=(b_rstd,))
            pr.op(act, A(nc.scalar.activation, out=rstd_t[:], in_=rstd_t[:], func=AF.Ln, bias=eps_ln, scale=1.0),
                  reads=(b_rstd, b_sm), writes=(b_rstd,))
            pr.op(act, A(nc.scalar.activation, out=rstd_t[:], in_=rstd_t[:], func=AF.Exp, scale=-0.5),
                  reads=(b_rstd,), writes=(b_rstd,))

        def stage_norm(tt):
            tsl = slice(tt * 512, (tt + 1) * 512)
            j = tt % 2
            rr = X[j]
            for oc in range(8):
                pr.op(dve, A(nc.vector.tensor_tensor, out=sqq[:, oc, :], in0=rr[:, oc, :], in1=mean_t[:], op=ALU.subtract),
                      reads=(b_X[j][oc], b_mean), writes=(b_sqq[oc],))
                pr.op(dve, A(nc.vector.tensor_tensor, out=sqq[:, oc, :], in0=sqq[:, oc, :], in1=rstd_t[:], op=ALU.mult),
                      reads=(b_sqq[oc], b_rstd), writes=(b_sqq[oc],))
                pr.op(act, A(nc.scalar.activation, out=sqq[:, oc, :], in_=sqq[:, oc, :], func=AF.Identity,
                             scale=lng_t[:, oc:oc + 1], bias=lnb_t[:, oc:oc + 1]),
                      reads=(b_sqq[oc], b_const), writes=(b_sqq[oc],))
            pr.dma(sp, [(outT[s].rearrange("(c p) n -> p c n", p=128)[:, :, tsl], sqq[:, :, :])], s_out, reads=tuple(b_sqq))

        load_res(0)
        if NT > 1:
            load_res(1)
        load_p(0)
        phase_pb()
        chk("pb")
        stage_wo(0)
        for tt in range(NT):
            stage_gate(tt)
            if tt + 1 < NT:
                stage_wo(tt + 1)
            stage_stats(tt)
            stage_norm(tt)
            if s + 1 < NSEQ:
                nsl = max(1, NT - 1)
                per = (8 + nsl - 1) // nsl
                ks = range(tt * per, min(8, (tt + 1) * per))
                if len(ks):
                    load_x(s + 1, ks)
            if tt + 2 < NT:
                load_res(tt + 2)

    try:
        for s_ in range(NSEQ):
            run_seq(s_)
    except _Stop:
        pass

    pr._waits(sp, {s_out: s_out.count})

    with nc.Block() as block:
        @block.tensor
        def _(e):
            for f in pe.prog:
                f()

        @block.scalar
        def _(e):
            for f in act.prog:
                f()

        @block.vector
        def _(e):
            for f in dve.prog:
                f()

        @block.gpsimd
        def _(e):
            for f in pool.prog:
                f()

        @block.sync
        def _(e):
            for f in sp.prog:
                f()

    return nc, pr


def make_consts(S):
    pos = np.arange(S)
    ph = (pos // 128).astype(np.float32)
    pl = (pos % 128).astype(np.float32)
    one = np.ones(S, np.float32)
    qaug = np.stack([-128.0 * ph, -pl, one, one]).astype(np.float32)
    kaug = np.stack([one, one, 128.0 * ph, pl]).astype(np.float32)
    i = np.arange(128)
    dtab = (-2.0 * np.maximum(i[:, None] - i[None, :], 0)).astype(np.float32)
    ident = np.eye(128, dtype=np.float32)
    return {"c_qaug": qaug, "c_kaug": kaug, "c_kaugn": np.ascontiguousarray(-kaug), "c_dtab": dtab, "c_ident": ident}


def host_layout(inputs, S, NSEQ, n_cores):
    f = lambda a: np.ascontiguousarray(np.asarray(a, dtype=np.float32))
    x = np.asarray(inputs["x"], dtype=np.float32)
    p = np.asarray(inputs["p"], dtype=np.float32)[0]
    shared = {
        "w_in": f(inputs["w_in"][0]), "w_pa": f(inputs["w_proj_a"][0]), "w_pb": f(inputs["w_proj_b"][0]),
        "w_out": f(inputs["w_out"][0]), "w_pg": f(inputs["w_ple_gate"][0]), "w_ple": f(inputs["w_ple"][0]),
        "cw": f(np.asarray(inputs["conv_w"])[0].reshape(3, 8, 128).transpose(2, 1, 0)),
        "cb": f(np.asarray(inputs["conv_b"])[0].reshape(8, 128).T),
        "subg": f(np.asarray(inputs["subln_g"])[0].reshape(128, 1)),
        "lng": f(np.asarray(inputs["ln_g"])[0].reshape(8, 128).T),
        "lnb": f(np.asarray(inputs["ln_b"])[0].reshape(8, 128).T),
        "lamv": f(np.broadcast_to(np.stack([np.asarray(inputs[k])[0] for k in
                                            ("lambda_q1", "lambda_k1", "lambda_q2", "lambda_k2")])[None], (128, 4, 64))),
    }
    shared.update(make_consts(S))
    in_maps = []
    for c in range(n_cores):
        m = dict(shared)
        m["xT"] = np.ascontiguousarray(x[c * NSEQ:(c + 1) * NSEQ].transpose(0, 2, 1))
        m["pT"] = np.ascontiguousarray(p[c * NSEQ:(c + 1) * NSEQ].transpose(0, 2, 1))
        in_maps.append(m)
    return in_maps


_CACHE = {}


def kernel(**inputs):
    S, NSEQ = 2048, 4
    in_maps = host_layout(inputs, S, NSEQ, N_CORES)
    if "nc" not in _CACHE:
        _CACHE["nc"] = build(S, NSEQ)[0]
    res = run_bass_kernel_spmd(_CACHE["nc"], in_maps, core_ids=list(range(N_CORES)))
    outs = [r["outT"].transpose(0, 2, 1) for r in res.results]
    return np.ascontiguousarray(np.concatenate(outs, axis=0)).astype(np.float32)
```

```python
import math
from contextlib import ExitStack

import numpy as np
import concourse.bass as bass
import concourse.mybir as mybir
from concourse.bass_utils import run_bass_kernel_spmd

F32 = mybir.dt.float32
BF16 = mybir.dt.bfloat16
AF = mybir.ActivationFunctionType
ALU = mybir.AluOpType
AX = mybir.AxisListType

D_MODEL = 1024
N_HEADS = 8
LN_EPS = 1e-5
RMS_EPS = 1e-5
ALPHA = 2.0 ** 0.25
LAMBDA_INIT = 0.8 - 0.6 * math.exp(0.0)
N_CORES = 8
NW = 3


def A(fn, *args, **kw):
    return (fn, args, kw)


class SemC:
    def __init__(self, sem, name):
        self.sem = sem
        self.name = name
        self.count = 0


class Eng:
    def __init__(self, name, h, semc, is_pe=False):
        self.name = name
        self.h = h
        self.semc = semc
        self.is_pe = is_pe
        self.waited = {}
        self.prog = []


class Buf:
    def __init__(self, name, excl=False):
        self.name = name
        self.excl = excl
        self.w = {}
        self.r = {}


class Prog:
    def __init__(self, nc):
        self.nc = nc
        self.stack = ExitStack()
        self.semcs = []
        self.pe = Eng("pe", nc.tensor, self.new_sem("c_pe"), is_pe=True)
        self.act = Eng("act", nc.scalar, self.new_sem("c_act"))
        self.dve = Eng("dve", nc.vector, self.new_sem("c_dve"))
        self.pool = Eng("pool", nc.gpsimd, self.new_sem("c_pool"))
        self.sp = Eng("sp", nc.sync, self.new_sem("c_sp"))
        self.engs = [self.pe, self.act, self.dve, self.pool, self.sp]

    def new_sem(self, name):
        sem = self.stack.enter_context(self.nc.semaphore(name))
        sc = SemC(sem, name)
        self.semcs.append(sc)
        return sc

    @staticmethod
    def _merge(d, src):
        for sc, v in src.items():
            if d.get(sc, 0) < v:
                d[sc] = v

    def _deps(self, eng, reads, writes):
        d = {}
        for b in reads:
            self._merge(d, b.w)
            if b.excl:
                self._merge(d, {sc: v for sc, v in b.r.items() if sc is not eng.semc})
        for b in writes:
            self._merge(d, b.w)
            self._merge(d, b.r)
        return d

    def _waits(self, eng, d):
        for sc, v in d.items():
            if eng.is_pe and sc is eng.semc:
                continue
            if eng.waited.get(sc, 0) >= v:
                continue
            eng.waited[sc] = v
            eng.prog.append(lambda h=eng.h, s=sc.sem, v=v: h.wait_ge(s, v))

    @staticmethod
    def _record(sc, v, reads, writes):
        for b in reads:
            if b.r.get(sc, 0) < v:
                b.r[sc] = v
        for b in writes:
            b.w = {sc: v}
            b.r = {}

    def op(self, eng, fn, reads=(), writes=()):
        self.group(eng, [fn], reads, writes)

    def group(self, eng, fns, reads=(), writes=()):
        self._waits(eng, self._deps(eng, reads, writes))
        sc = eng.semc
        sc.count += 1
        v = sc.count
        for (f, a, kw) in fns[:-1]:
            eng.prog.append(lambda f=f, a=a, kw=kw: f(*a, **kw))
        (f, a, kw) = fns[-1]
        eng.prog.append(lambda f=f, a=a, kw=kw, s=sc.sem: f(*a, **kw).then_inc(s, 1))
        self._record(sc, v, reads, writes)

    def dma(self, eng, pairs, semc, reads=(), writes=()):
        self._waits(eng, self._deps(eng, reads, writes))
        for (o, i) in pairs:
            semc.count += 16
            eng.prog.append(lambda h=eng.h, o=o, i=i, s=semc.sem: h.dma_start(out=o, in_=i).then_inc(s, 16))
        self._record(semc, semc.count, reads, writes)

    def frontier(self):
        return {sc: sc.count for sc in self.semcs if sc.count > 0}

    def fresh(self, name):
        b = Buf(name)
        b.w = self.frontier()
        return b


class Region:
    def __init__(self, handle, words):
        self.handle = handle
        self.words = words
        self.off = 0

    def reset(self):
        self.off = 0

    def take(self, free_shape, dt):
        n = int(np.prod(free_shape))
        words = n if dt == F32 else (n + 1) // 2
        assert self.off + words <= self.words, ("region overflow", self.off, words, self.words)
        v = self.handle[:, self.off:self.off + words]
        self.off += words
        if dt != F32:
            v = v.bitcast(dt)
        if len(free_shape) == 2:
            v = v.rearrange("p (a b) -> p a b", a=free_shape[0])
        return v


class _Stop(Exception):
    pass


BAND_T = 144.0


def _min_dist(qt, kc):
    q_lo, q_hi = 512 * qt, 512 * qt + 511
    k_lo, k_hi = 128 * kc, 128 * kc + 127
    if k_lo > q_hi:
        return k_lo - q_hi
    if k_hi < q_lo:
        return q_lo - k_hi
    return 0


def build(S, NSEQ, dbg_stop=None):
    def chk(name):
        if dbg_stop == name:
            raise _Stop()

    NT = S // 512
    NKC = S // 128
    nc = bass.Bass("TRN2", target_bir_lowering=False)

    def din(name, shape):
        return nc.dram_tensor(name, list(shape), F32, kind="ExternalInput").ap()

    xT = din("xT", [NSEQ, 1024, S])
    pT = din("pT", [NSEQ, 256, S])
    w_in = din("w_in", [1024, 10240])
    w_pa = din("w_pa", [1024, 1024])
    w_pb = din("w_pb", [1024, 1024])
    w_out = din("w_out", [1024, 1024])
    w_pg = din("w_pg", [1024, 1024])
    w_ple = din("w_ple", [256, 1024])
    d_cw = din("cw", [128, 8, 3])
    d_cb = din("cb", [128, 8])
    d_subg = din("subg", [128, 1])
    d_lng = din("lng", [128, 8])
    d_lnb = din("lnb", [128, 8])
    d_lamv = din("lamv", [128, 4, 64])
    d_qaug = din("c_qaug", [4, S])
    d_kaug = din("c_kaug", [4, S])
    d_kaugn = din("c_kaugn", [4, S])
    d_dtab = din("c_dtab", [128, 128])
    d_ident = din("c_ident", [128, 128])
    outT = nc.dram_tensor("outT", [NSEQ, 1024, S], F32, kind="ExternalOutput").ap()

    pr = Prog(nc)
    st = pr.stack
    pe, act, dve, pool, sp = pr.pe, pr.act, pr.dve, pr.pool, pr.sp

    def sb(name, shape, dt):
        return st.enter_context(nc.sbuf_tensor(name, list(shape), dt))

    xbf = sb("xbf", [128, 8, S], BF16)
    ybuf = sb("ybuf", [128, 8, S], BF16)
    mabuf = sb("mabuf", [128, 8, S], BF16)
    wbuf = sb("wbuf", [128, NW, 8, 512], BF16)
    REGW = max(8 * S + 4 + 1024, 7 * S + 4608, 18432) + 64
    reg_t = sb("reg", [128, REGW], F32)
    region = Region(reg_t, REGW)
    ones_bf = sb("ones_bf", [128, 128], BF16)
    ones_f = sb("ones_f", [128, 128], F32)
    ident = sb("ident", [128, 128], BF16)
    dtab = sb("dtab", [128, 128], BF16)
    cw_t = sb("cw_t", [128, 8, 3], F32)
    cb_t = sb("cb_t", [128, 8], F32)
    subg_t = sb("subg_t", [128, 1], F32)
    lng_t = sb("lng_t", [128, 8], F32)
    lnb_t = sb("lnb_t", [128, 8], F32)
    lamv_t = sb("lamv_t", [128, 4, 64], F32)
    wple_t = sb("wple_t", [128, 2, 1024], BF16)
    sgt = [sb(f"sgt{i}", [128, 512], F32) for i in range(2)]
    _tgt0 = sb("tgt0", [128, 512], F32)
    tgt = [_tgt0, _tgt0]
    b_sg = [Buf("sg0"), Buf("sg1")]
    _b_tg0 = Buf("tg0")
    b_tg = [_b_tg0, _b_tg0]
    sm = sb("small", [128, 16], F32)
    ps = st.enter_context(nc.psum_tensor("ps", [128, 8, 512], F32))

    banks = [Buf(f"bank{i}", excl=True) for i in range(8)]
    b_const = Buf("const")
    b_xbf = Buf("xbf")
    b_y = [Buf(f"y{i}") for i in range(8)]
    b_ma = [Buf(f"ma{i}") for i in range(8)]
    b_w = [Buf(f"w{i}") for i in range(NW)]
    s_w = [pr.new_sem(f"d_w{i}") for i in range(NW)]
    s_setup = pr.new_sem("d_setup")
    s_setup2 = pr.new_sem("d_setup2")
    b_const2 = Buf("const2")
    s_xbf = pr.new_sem("d_xbf")
    s_aug = pr.new_sem("d_aug")
    s_xres = [pr.new_sem("d_xres0"), pr.new_sem("d_xres1")]
    s_ptb = pr.new_sem("d_ptb")
    s_out = pr.new_sem("d_out")

    ring = {"i": 0}

    def next_bank(lo=0, hi=8):
        n = hi - lo
        i = lo + (ring["i"] % n)
        ring["i"] += 1
        return i

    def bank_ap(i, rows=slice(0, 128), cols=slice(0, 512)):
        return ps[rows, i, cols]

    def wv(src, c0, n):
        return src[:, c0:c0 + n].rearrange("(c p) n -> p c n", p=128)

    plan = []
    for s in range(NSEQ):
        for cc in range(8):
            plan.append((("conv", s, cc),
                         [(r * 128, 128, 8, wv(w_in, r * 1024 + cc * 128, 128)) for r in range(4)]))
        for oc in range(8):
            plan.append((("pa", s, oc),
                         [(0, 128, 8, wv(w_in, 8192 + oc * 128, 128)),
                          (128, 128, 8, wv(w_pa, oc * 128, 128))]))
        for h in range(8):
            if h % 4 == 0:
                plan.append((("v", s, h // 4), [(0, 512, 8, wv(w_in, 6144 + (h // 4) * 512, 512))]))
            plan.append((("qkz", s, h),
                         [(0, 128, 8, wv(w_in, 4096 + h * 128, 128)),
                          (128, 128, 8, wv(w_in, 5120 + h * 128, 128)),
                          (256, 128, 8, wv(w_in, 7168 + h * 128, 128))]))
        for oc in range(8):
            plan.append((("pb", s, oc),
                         [(0, 128, 8, wv(w_in, 9216 + oc * 128, 128)),
                          (128, 128, 8, wv(w_pb, oc * 128, 128))]))
        for tt in range(NT):
            for g in range(2):
                plan.append((("wo", s, tt, g), [(0, 512, 8, wv(w_out, g * 512, 512))]))
            for g in range(2):
                plan.append((("pg", s, tt, g), [(0, 512, 8, wv(w_pg, g * 512, 512))]))

    wstate = {"issued": 0, "next": 0}

    def w_issue(i):
        key, segs = plan[i]
        slot = i % NW
        pairs = []
        for (c0, n, nk, src) in segs:
            if nk == 8:
                pairs.append((wbuf[:, slot, :, c0:c0 + n], src))
            elif nk == 2:
                pairs.append((wbuf[:, slot, 0:2, c0:c0 + n], src))
            else:
                pairs.append((wbuf[:, slot, 2:4, c0:c0 + n], src))
        pr.dma(pool, pairs, s_w[slot], reads=(), writes=(b_w[slot],))

    def w_next(key):
        i = wstate["next"]
        assert plan[i][0] == key, (plan[i][0], key)
        while wstate["issued"] < min(len(plan), i + NW):
            w_issue(wstate["issued"])
            wstate["issued"] += 1
        wstate["next"] += 1
        slot = i % NW
        return (lambda kc, c0, n: wbuf[:, slot, kc, c0:c0 + n]), b_w[slot]

    def mm_group(bank_i, lhs_fn, rhs_fn, nk, reads, rows=slice(0, 128), cols=slice(0, 512)):
        fns = []
        for kc in range(nk):
            fns.append(A(nc.tensor.matmul, bank_ap(bank_i, rows, cols), lhs_fn(kc), rhs_fn(kc),
                                                      start=(kc == 0), stop=(kc == nk - 1)))
        pr.group(pe, fns, reads=reads, writes=(banks[bank_i],))

    pr.dma(sp, [(cw_t[:], d_cw), (cb_t[:], d_cb), (subg_t[:], d_subg), (lng_t[:], d_lng),
                (lnb_t[:], d_lnb), (lamv_t[:], d_lamv)], s_setup, writes=(b_const,))
    pr.dma(pool, [(ident[:], d_ident), (dtab[:], d_dtab),
                  (wple_t[:, :, :], w_ple.rearrange("(c p) n -> p c n", p=128))], s_setup2, writes=(b_const2,))
    pr.op(dve, A(nc.vector.memset, ones_bf[:], 1.0), writes=(b_const,))
    pr.op(dve, A(nc.vector.memset, ones_f[:], 1.0), writes=(b_const,))
    b_sm = Buf("sm")
    lprod = sb("lprod", [128, 2, 64], F32)
    pr.op(dve, A(nc.vector.tensor_tensor, out=lprod[:, 0, :], in0=lamv_t[:, 0, :], in1=lamv_t[:, 1, :], op=ALU.mult),
          reads=(b_const,), writes=(b_sm,))
    pr.op(dve, A(nc.vector.tensor_tensor, out=lprod[:, 1, :], in0=lamv_t[:, 2, :], in1=lamv_t[:, 3, :], op=ALU.mult),
          reads=(b_const,), writes=(b_sm,))
    pr.op(dve, A(nc.vector.tensor_reduce, out=sm[:, 0:1], in_=lprod[:, 0, :], axis=AX.X, op=ALU.add),
          reads=(b_sm,), writes=(b_sm,))
    pr.op(dve, A(nc.vector.tensor_reduce, out=sm[:, 1:2], in_=lprod[:, 1, :], axis=AX.X, op=ALU.add),
          reads=(b_sm,), writes=(b_sm,))
    pr.op(act, A(nc.scalar.activation, out=sm[:, 2:4], in_=sm[:, 0:2], func=AF.Exp), reads=(b_sm,), writes=(b_sm,))
    pr.op(dve, A(nc.vector.tensor_tensor, out=sm[:, 4:5], in0=sm[:, 2:3], in1=sm[:, 3:4], op=ALU.subtract),
          reads=(b_sm,), writes=(b_sm,))
    pr.op(dve, A(nc.vector.tensor_scalar, out=sm[:, 5:6], in0=sm[:, 4:5], scalar1=float(LAMBDA_INIT), scalar2=-1.0,
                                               op0=ALU.add, op1=ALU.mult), reads=(b_sm,), writes=(b_sm,))
    pr.op(dve, A(nc.vector.tensor_scalar, out=sm[:, 6:7], in0=subg_t[:, 0:1],
                                               scalar1=float((1.0 - LAMBDA_INIT) * math.sqrt(128.0)), scalar2=0.0,
                                               op0=ALU.mult, op1=ALU.add), reads=(b_const, b_sm), writes=(b_sm,))
    pr.op(dve, A(nc.vector.memset, sm[:, 7:8], float(128.0 * RMS_EPS)), writes=(b_sm,))
    pr.op(dve, A(nc.vector.memset, sm[:, 8:9], float(LN_EPS)), writes=(b_sm,))
    neglam = sm[:, 5:6]
    geff = sm[:, 6:7]
    eps_rms = sm[:, 7:8]
    eps_ln = sm[:, 8:9]

    def load_x(s, kcs_=range(8)):
        xv = xT[s].rearrange("(c p) n -> p c n", p=128)
        pr.dma(pool, [(xbf[:, kc, :], xv[:, kc, :]) for kc in kcs_], s_xbf, writes=(b_xbf,))

    load_x(0)

    def run_seq(s):
        region.reset()
        hb = [region.take((S + 2,), F32) for _ in range(2)]
        bgb = [region.take((S,), F32) for _ in range(2)]
        zsb_c = [region.take((S,), F32) for _ in range(2)]
        tb = [region.take((S,), F32) for _ in range(2)]
        usb = [region.take((512,), F32) for _ in range(2)]
        b_h = [pr.fresh("h0"), pr.fresh("h1")]
        b_bg = [pr.fresh("bg0"), pr.fresh("bg1")]
        b_zs = [pr.fresh("zs0"), pr.fresh("zs1")]
        b_t = [pr.fresh("t0"), pr.fresh("t1")]
        b_u = [pr.fresh("u0"), pr.fresh("u1")]
        for k in range(2):
            pr.op(dve, A(nc.vector.memset, hb[k][:, 0:1], 0.0), writes=(b_h[k],))
            pr.op(dve, A(nc.vector.memset, hb[k][:, S + 1:S + 2], 0.0), writes=(b_h[k],))
        ucount = 0
        for cc in range(8):
            k = cc % 2
            wf, wb = w_next(("conv", s, cc))
            for tt in range(NT):
                tsl = slice(tt * 512, (tt + 1) * 512)
                bi = [next_bank() for _ in range(4)]
                for j in range(4):
                    mm_group(bi[j], lambda kc, j=j: wf(kc, j * 128, 128), lambda kc: xbf[:, kc, tsl], 8,
                             reads=(wb, b_xbf))
                uk = ucount % 2
                ucount += 1
                pr.op(act, A(nc.scalar.activation, out=usb[uk][:], in_=bank_ap(bi[0]), func=AF.Identity),
                      reads=(banks[bi[0]],), writes=(b_u[uk],))
                pr.op(dve, A(nc.vector.tensor_tensor,
                    out=hb[k][:, 1 + tt * 512:1 + (tt + 1) * 512], in0=bank_ap(bi[1]), in1=usb[uk][:], op=ALU.mult),
                    reads=(banks[bi[1]], b_u[uk]), writes=(b_h[k],))
                pr.op(act, A(nc.scalar.activation, out=bgb[k][:, tsl], in_=bank_ap(bi[2]), func=AF.Identity),
                      reads=(banks[bi[2]],), writes=(b_bg[k],))
                pr.op(act, A(nc.scalar.activation, out=zsb_c[k][:, tsl], in_=bank_ap(bi[3]), func=AF.Silu),
                      reads=(banks[bi[3]],), writes=(b_zs[k],))
            pr.op(act, A(nc.scalar.activation, out=tb[k][:], in_=hb[k][:, 1:S + 1], func=AF.Identity,
                                                               scale=cw_t[:, cc, 1:2], bias=cb_t[:, cc:cc + 1]),
                  reads=(b_h[k], b_const), writes=(b_t[k],))
            pr.op(dve, A(nc.vector.scalar_tensor_tensor, out=tb[k][:], in0=hb[k][:, 0:S], scalar=cw_t[:, cc, 0:1],
                                                                        in1=tb[k][:], op0=ALU.mult, op1=ALU.add),
                  reads=(b_h[k], b_const, b_t[k]), writes=(b_t[k],))
            pr.op(dve, A(nc.vector.scalar_tensor_tensor, out=tb[k][:], in0=hb[k][:, 2:S + 2], scalar=cw_t[:, cc, 2:3],
                                                                        in1=tb[k][:], op0=ALU.mult, op1=ALU.add),
                  reads=(b_h[k], b_const, b_t[k]), writes=(b_t[k],))
            pr.op(dve, A(nc.vector.tensor_tensor, out=tb[k][:], in0=tb[k][:], in1=bgb[k][:], op=ALU.mult),
                  reads=(b_t[k], b_bg[k]), writes=(b_t[k],))
            pr.op(dve, A(nc.vector.tensor_tensor, out=ybuf[:, cc, :], in0=tb[k][:], in1=zsb_c[k][:], op=ALU.mult),
                  reads=(b_t[k], b_zs[k]), writes=(b_y[cc],))

        chk("conv")
        gcount = 0
        for oc in range(8):
            wf, wb = w_next(("pa", s, oc))
            for tt in range(NT):
                tsl = slice(tt * 512, (tt + 1) * 512)
                bg_i = next_bank()
                by_i = next_bank()
                mm_group(bg_i, lambda kc: wf(kc, 0, 128), lambda kc: xbf[:, kc, tsl], 8, reads=(wb, b_xbf))
                mm_group(by_i, lambda kc: wf(kc, 128, 128), lambda kc: ybuf[:, kc, tsl], 8, reads=(wb,) + tuple(b_y))
                k = gcount % 2
                gcount += 1
                pr.op(act, A(nc.scalar.activation, out=sgt[k][:], in_=bank_ap(bg_i), func=AF.Sigmoid),
                      reads=(banks[bg_i],), writes=(b_sg[k],))
                pr.op(dve, A(nc.vector.tensor_tensor,
                    out=mabuf[:, oc, tsl], in0=bank_ap(by_i), in1=sgt[k][:], op=ALU.mult),
                    reads=(banks[by_i], b_sg[k]), writes=(b_ma[oc],))

        chk("pa")
        region.reset()
        QA = region.take((S,), BF16)
        QB = region.take((S,), BF16)
        KAp = region.take((S,), BF16)
        KAn = region.take((S,), BF16)
        KBp = region.take((S,), BF16)
        KBn = region.take((S,), BF16)
        V4 = region.take((NKC, 512), BF16)
        zsb2 = [region.take((S,), F32) for _ in range(2)]
        Et = [region.take((2, 512), BF16) for _ in range(3)]
        r_t = [region.take((512,), F32) for _ in range(2)]
        o_t = [region.take((512,), F32) for _ in range(2)]
        sq_t = region.take((512,), F32)
        rs_t = region.take((512,), F32)
        b_QA, b_QB = pr.fresh("QA"), pr.fresh("QB")
        b_KAp, b_KAn, b_KBp, b_KBn = pr.fresh("KAp"), pr.fresh("KAn"), pr.fresh("KBp"), pr.fresh("KBn")
        b_V4 = pr.fresh("V4")
        b_zsb2 = [pr.fresh("zsb0"), pr.fresh("zsb1")]
        b_E = [pr.fresh(f"E{i}") for i in range(3)]
        b_r = [pr.fresh("r0"), pr.fresh("r1")]
        b_o = [pr.fresh("o0"), pr.fresh("o1")]
        b_sq = pr.fresh("sq")
        b_rs = pr.fresh("rs")
        for (t_, b_) in ((QB, b_QB), (KBp, b_KBp), (KBn, b_KBn)):
            pr.op(dve, A(nc.vector.memset, t_[0:64, :], 0.0), writes=(b_,))
        pr.dma(pool, [(QA[64:68, :], d_qaug), (QB[0:4, :], d_qaug), (KAp[64:68, :], d_kaug), (KAn[64:68, :], d_kaugn),
                      (KBp[0:4, :], d_kaug), (KBn[0:4, :], d_kaugn)], s_aug,
               writes=(b_QA, b_QB, b_KAp, b_KAn, b_KBp, b_KBn))
        chk("attn_setup")
        SB = lambda m, par: 2 * par + m
        OB = lambda m: 4 + m
        ZB = lambda m: 6 + m
        pending = [None, None]
        for h in range(8):
            zsb, b_zsb = zsb2[h % 2], b_zsb2[h % 2]
            mh = 2.0 ** (-(h + 1))
            qscale = 0.125 / mh
            hh = h % 4
            if hh == 0:
                wfv, wbv = w_next(("v", s, h // 4))
                for tc in range(NKC):
                    bi = next_bank(0, 4)
                    mm_group(bi, lambda kc, tc=tc: xbf[:, kc, tc * 128:(tc + 1) * 128], lambda kc: wfv(kc, 0, 512), 8,
                             reads=(wbv, b_xbf))
                    if tc % 2 == 0:
                        pr.op(act, A(nc.scalar.activation, out=V4[:, tc, :], in_=bank_ap(bi), func=AF.Identity),
                              reads=(banks[bi],), writes=(b_V4,))
                    else:
                        pr.op(dve, A(nc.vector.tensor_copy, out=V4[:, tc, :], in_=bank_ap(bi)),
                              reads=(banks[bi],), writes=(b_V4,))
            chk("attn_v")
            wf, wb = w_next(("qkz", s, h))
            lo, hi = slice(0, 64), slice(64, 128)
            for tt in range(NT):
                tsl = slice(tt * 512, (tt + 1) * 512)
                bq = next_bank(0, 4)
                mm_group(bq, lambda kc: wf(kc, 0, 128), lambda kc: xbf[:, kc, tsl], 8, reads=(wb, b_xbf))
                pr.op(act, A(nc.scalar.activation, out=QA[lo, tsl], in_=bank_ap(bq, lo), func=AF.Identity,
                                                                      scale=float(qscale)),
                      reads=(banks[bq],), writes=(b_QA,))
                pr.op(dve, A(nc.vector.tensor_scalar, out=QB[hi, tsl], in0=bank_ap(bq, hi), scalar1=float(qscale),
                                                                        scalar2=0.0, op0=ALU.mult, op1=ALU.add),
                      reads=(banks[bq],), writes=(b_QB,))
                chk("attn_q")
                bk = next_bank(0, 4)
                mm_group(bk, lambda kc: wf(kc, 128, 128), lambda kc: xbf[:, kc, tsl], 8, reads=(wb, b_xbf))
                pr.op(act, A(nc.scalar.activation, out=KAp[lo, tsl], in_=bank_ap(bk, lo), func=AF.Identity),
                      reads=(banks[bk],), writes=(b_KAp,))
                pr.op(dve, A(nc.vector.tensor_copy, out=KAn[lo, tsl], in_=bank_ap(bk, lo)),
                      reads=(banks[bk],), writes=(b_KAn,))
                pr.op(act, A(nc.scalar.activation, out=KBp[hi, tsl], in_=bank_ap(bk, hi), func=AF.Identity),
                      reads=(banks[bk],), writes=(b_KBp,))
                pr.op(dve, A(nc.vector.tensor_copy, out=KBn[hi, tsl], in_=bank_ap(bk, hi)),
                      reads=(banks[bk],), writes=(b_KBn,))
                chk("attn_k")
                bz = next_bank(0, 4)
                mm_group(bz, lambda kc: wf(kc, 256, 128), lambda kc: xbf[:, kc, tsl], 8, reads=(wb, b_xbf))
                pr.op(act, A(nc.scalar.activation, out=zsb[:, tsl], in_=bank_ap(bz), func=AF.Silu),
                      reads=(banks[bz],), writes=(b_zsb,))

            chk("attn_proj")
            maps = [
                (QA[0:68, :], KAp[0:68, :], KAn[0:68, :], QA[0:64, :], KAp[0:64, :], (b_QA, b_KAp, b_KAn)),
                (QB[0:128, :], KBp[0:128, :], KBn[0:128, :], QB[64:128, :], KBp[64:128, :], (b_QB, b_KBp, b_KBn)),
            ]

            for qt in range(NT):
                q0 = qt * 512
                kcs = [kc for kc in range(NKC) if mh * _min_dist(qt, kc) < BAND_T]
                assert len(kcs) >= 4

                def qk(kc, q0=q0, qt=qt):
                    k0 = kc * 128
                    main, fix, rd, wr = [], [], (), ()
                    for m in range(2):
                        Qa, Kp, Kn, Qd, Kd, bufs = maps[m]
                        bi = SB(m, kc % 2)
                        rd += bufs
                        wr += (banks[bi],)
                        if kc < 4 * qt:
                            main.append(A(nc.tensor.matmul,
                                bank_ap(bi), Kp[:, k0:k0 + 128], Qa[:, q0:q0 + 512], start=True, stop=True))
                        elif kc >= 4 * qt + 4:
                            main.append(A(nc.tensor.matmul,
                                bank_ap(bi), Kn[:, k0:k0 + 128], Qa[:, q0:q0 + 512], start=True, stop=True))
                        else:
                            j = kc - 4 * qt
                            if j > 0:
                                main.append(A(nc.tensor.matmul,
                                    bank_ap(bi, cols=slice(0, 128 * j)), Kn[:, k0:k0 + 128], Qa[:, q0:q0 + 128 * j],
                                    start=True, stop=True))
                            main.append(A(nc.tensor.matmul,
                                bank_ap(bi, cols=slice(128 * j, 512)), Kp[:, k0:k0 + 128],
                                Qa[:, q0 + 128 * j:q0 + 512], start=True, stop=False, skip_group_check=True))
                            fix.append(A(nc.tensor.matmul,
                                bank_ap(bi, cols=slice(128 * j, 128 * j + 128)), ident[:], dtab[:], start=False, stop=True,
                                skip_group_check=True))
                    pr.group(pe, main + fix, reads=rd + (b_const, b_const2), writes=wr)
                    e = kc % 3
                    par = kc % 2
                    pr.op(act, A(nc.scalar.activation, out=Et[e][:, :, :], in_=ps[:, 2 * par:2 * par + 2, :], func=AF.Exp,
                                 scale=float(mh)),
                          reads=(banks[SB(0, par)], banks[SB(1, par)]), writes=(b_E[e],))

                def pv(kc):
                    e = kc % 3
                    for m in range(2):
                        fns = [
                            A(nc.tensor.matmul, bank_ap(OB(m)), V4[:, kc, hh * 128:(hh + 1) * 128], Et[e][:, m, :],
                                                              start=(kc == kcs[0]), stop=(kc == kcs[-1])),
                            A(nc.tensor.matmul, bank_ap(ZB(m)), ones_bf[:], Et[e][:, m, :],
                                                              start=(kc == kcs[0]), stop=(kc == kcs[-1])),
                        ]
                        pr.group(pe, fns, reads=(b_E[e], b_V4, b_const), writes=(banks[OB(m)], banks[ZB(m)]))

                for i, kc in enumerate(kcs):
                    qk(kc)
                    chk("attn_qk")
                    if i >= 1:
                        pv(kcs[i - 1])
                        chk("attn_pv")
                    if i == 1 and pending[0] is not None:
                        pending[0]()
                        pending[0] = None
                    if i == 3 and pending[1] is not None:
                        pending[1](par=(kcs[4] % 2 if len(kcs) > 4 else 0))
                        pending[1] = None
                pv(kcs[-1])
                chk("attn_tile")
                for m in range(2):
                    pr.op(act, A(nc.scalar.activation, out=r_t[m][:], in_=bank_ap(ZB(m)), func=AF.Ln),
                          reads=(banks[ZB(m)],), writes=(b_r[m],))
                    pr.op(dve, A(nc.vector.tensor_copy, out=o_t[m][:], in_=bank_ap(OB(m))),
                          reads=(banks[OB(m)],), writes=(b_o[m],))

                def postB():
                    for m in range(2):
                        pr.op(act, A(nc.scalar.activation, out=r_t[m][:], in_=r_t[m][:], func=AF.Exp, scale=-1.0),
                              reads=(b_r[m],), writes=(b_r[m],))
                    for m in range(2):
                        pr.op(dve, A(nc.vector.tensor_tensor, out=o_t[m][:], in0=o_t[m][:], in1=r_t[m][:], op=ALU.mult),
                              reads=(b_o[m], b_r[m]), writes=(b_o[m],))
                    pr.op(dve, A(nc.vector.scalar_tensor_tensor, out=o_t[0][:], in0=o_t[1][:], scalar=neglam, in1=o_t[0][:],
                                 op0=ALU.mult, op1=ALU.add),
                          reads=(b_o[0], b_o[1], b_sm), writes=(b_o[0],))
                    pr.op(dve, A(nc.vector.tensor_tensor, out=sq_t[:], in0=o_t[0][:], in1=o_t[0][:], op=ALU.mult),
                          reads=(b_o[0],), writes=(b_sq,))

                def postC(h=h, qt=qt, par=0, zsb=zsb, b_zsb=b_zsb):
                    bi = SB(0, par)
                    pr.group(pe, [A(nc.tensor.matmul, bank_ap(bi), ones_f[:], sq_t[:], start=True, stop=True)],
                             reads=(b_sq, b_const), writes=(banks[bi],))
                    pr.op(act, A(nc.scalar.activation, out=rs_t[:], in_=bank_ap(bi), func=AF.Ln, bias=eps_rms, scale=1.0),
                          reads=(banks[bi], b_sm), writes=(b_rs,))
                    pr.op(act, A(nc.scalar.activation, out=rs_t[:], in_=rs_t[:], func=AF.Exp, scale=-0.5),
                          reads=(b_rs,), writes=(b_rs,))
                    pr.op(dve, A(nc.vector.tensor_tensor, out=o_t[0][:], in0=o_t[0][:], in1=rs_t[:], op=ALU.mult),
                          reads=(b_o[0], b_rs), writes=(b_o[0],))
                    pr.op(dve, A(nc.vector.scalar_tensor_tensor,
                                 out=ybuf[:, h, qt * 512:(qt + 1) * 512], in0=o_t[0][:], scalar=geff,
                                 in1=zsb[:, qt * 512:(qt + 1) * 512], op0=ALU.mult, op1=ALU.mult),
                          reads=(b_o[0], b_zsb, b_sm), writes=(b_y[h],))

                chk("attn_post1")
                if qt == NT - 1 and h == 7:
                    postB()
                    postC()
                    chk("attn_post2")
                else:
                    pending[0] = postB
                    pending[1] = postC

        chk("attn")
        def phase_pb():
            gcount = 0
            for oc in range(8):
                wf, wb = w_next(("pb", s, oc))
                for tt in range(NT):
                    tsl = slice(tt * 512, (tt + 1) * 512)
                    bg_i = next_bank()
                    by_i = next_bank()
                    mm_group(bg_i, lambda kc: wf(kc, 0, 128), lambda kc: xbf[:, kc, tsl], 8, reads=(wb, b_xbf))
                    mm_group(by_i, lambda kc: wf(kc, 128, 128), lambda kc: ybuf[:, kc, tsl], 8, reads=(wb,) + tuple(b_y))
                    k = gcount % 2
                    gcount += 1
                    pr.op(act, A(nc.scalar.activation, out=sgt[k][:], in_=bank_ap(bg_i), func=AF.Sigmoid),
                          reads=(banks[bg_i],), writes=(b_sg[k],))
                    pr.op(dve, A(nc.vector.tensor_tensor, out=tgt[k][:], in0=bank_ap(by_i), in1=sgt[k][:], op=ALU.mult),
                          reads=(banks[by_i], b_sg[k]), writes=(b_tg[k],))
                    pr.op(dve, A(nc.vector.tensor_tensor, out=mabuf[:, oc, tsl], in0=tgt[k][:],
                                                                                   in1=mabuf[:, oc, tsl], op=ALU.add),
                          reads=(b_tg[k], b_ma[oc]), writes=(b_ma[oc],))


        region.reset()
        X = [region.take((8, 512), F32) for _ in range(2)]
        sqq = region.take((8, 512), F32)
        rbf = region.take((8, 512), BF16)
        ptb = region.take((2, 512), BF16)
        gs_t = [region.take((512,), F32) for _ in range(2)]
        tg_t = [region.take((512,), F32) for _ in range(2)]
        mean_t = region.take((512,), F32)
        msq_t = region.take((512,), F32)
        rstd_t = region.take((512,), F32)
        b_ptb = pr.fresh("ptb")
        b_X = [[pr.fresh(f"x{j}_{i}") for i in range(8)] for j in range(2)]
        b_sqq = [pr.fresh(f"sqq{i}") for i in range(8)]
        b_rbf = [pr.fresh(f"rbf{i}") for i in range(8)]
        b_gs = [pr.fresh("gs0"), pr.fresh("gs1")]
        b_tg2 = [pr.fresh("tg0"), pr.fresh("tg1")]
        b_mean, b_msq, b_rstd = pr.fresh("mean"), pr.fresh("msq"), pr.fresh("rstd")

        def load_res(tt):
            tsl = slice(tt * 512, (tt + 1) * 512)
            j = tt % 2
            pr.dma(sp, [(X[j][:, :, :], xT[s].rearrange("(c p) n -> p c n", p=128)[:, :, tsl])], s_xres[j],
                   writes=tuple(b_X[j]))

        def load_p(tt):
            tsl = slice(tt * 512, (tt + 1) * 512)
            pr.dma(pool, [(ptb[:, :, :], pT[s].rearrange("(c p) n -> p c n", p=128)[:, :, tsl])], s_ptb, writes=(b_ptb,))

        def stage_wo(tt):
            tsl = slice(tt * 512, (tt + 1) * 512)
            j = tt % 2
            rr = X[j]
            for g in range(2):
                wf, wb = w_next(("wo", s, tt, g))
                for oc in range(4 * g, 4 * g + 4):
                    bi = next_bank()
                    mm_group(bi, lambda kc, oc=oc: wf(kc, (oc % 4) * 128, 128), lambda kc: mabuf[:, kc, tsl], 8,
                             reads=(wb,) + tuple(b_ma))
                    pr.op(dve, A(nc.vector.scalar_tensor_tensor,
                                 out=rr[:, oc, :], in0=rr[:, oc, :], scalar=float(ALPHA), in1=bank_ap(bi),
                                 op0=ALU.mult, op1=ALU.add),
                          reads=(banks[bi], b_X[j][oc]), writes=(b_X[j][oc],))
                    pr.op(act, A(nc.scalar.activation, out=rbf[:, oc, :], in_=rr[:, oc, :], func=AF.Identity),
                          reads=(b_X[j][oc],), writes=(b_rbf[oc],))

        def stage_gate(tt):
            j = tt % 2
            rr = X[j]
            gcount = 0
            for g in range(2):
                wf, wb = w_next(("pg", s, tt, g))
                for oc in range(4 * g, 4 * g + 4):
                    bg_i = next_bank()
                    bp_i = next_bank()
                    mm_group(bg_i, lambda kc, oc=oc: wf(kc, (oc % 4) * 128, 128), lambda kc: rbf[:, kc, :], 8,
                             reads=(wb,) + tuple(b_rbf))
                    mm_group(bp_i, lambda kc, oc=oc: wple_t[:, kc, oc * 128:(oc + 1) * 128], lambda kc: ptb[:, kc, :], 2,
                             reads=(b_const2, b_ptb))
                    k = gcount % 2
                    gcount += 1
                    pr.op(act, A(nc.scalar.activation, out=gs_t[k][:], in_=bank_ap(bg_i), func=AF.Sigmoid),
                          reads=(banks[bg_i],), writes=(b_gs[k],))
                    pr.op(dve, A(nc.vector.tensor_tensor, out=tg_t[k][:], in0=bank_ap(bp_i), in1=gs_t[k][:], op=ALU.mult),
                          reads=(banks[bp_i], b_gs[k]), writes=(b_tg2[k],))
                    pr.op(dve, A(nc.vector.tensor_tensor, out=rr[:, oc, :], in0=rr[:, oc, :], in1=tg_t[k][:], op=ALU.add),
                          reads=(b_tg2[k], b_X[j][oc]), writes=(b_X[j][oc],))
            if tt + 1 < NT:
                load_p(tt + 1)

        def stage_stats(tt):
            j = tt % 2
            rr = X[j]
            bsum = next_bank()
            mm_group(bsum, lambda kc: ones_f[:], lambda kc: rr[:, kc, :], 8, reads=tuple(b_X[j]) + (b_const,))
            pr.op(act, A(nc.scalar.activation, out=sqq[:, :, :], in_=rr[:, :, :], func=AF.Square),
                  reads=tuple(b_X[j]), writes=tuple(b_sqq))
            bsq = next_bank()
            mm_group(bsq, lambda kc: ones_f[:], lambda kc: sqq[:, kc, :], 8, reads=tuple(b_sqq) + (b_const,))
            pr.op(dve, A(nc.vector.tensor_scalar, out=mean_t[:], in0=bank_ap(bsum), scalar1=1.0 / 1024.0, scalar2=0.0,
                         op0=ALU.mult, op1=ALU.add),
                  reads=(banks[bsum],), writes=(b_mean,))
            pr.op(dve, A(nc.vector.tensor_tensor, out=msq_t[:], in0=mean_t[:], in1=mean_t[:], op=ALU.mult),
                  reads=(b_mean,), writes=(b_msq,))
            pr.op(dve, A(nc.vector.scalar_tensor_tensor, out=rstd_t[:], in0=bank_ap(bsq), scalar=1.0 / 1024.0, in1=msq_t[:],
                         op0=ALU.mult, op1=ALU.subtract),
                  reads=(banks[bsq], b_msq), writes=(b_rstd,))
            pr.op(act, A(nc.scalar.activation, out=rstd_t[:], in_=rstd_t[:], func=AF.Ln, bias=eps_ln, scale=1.0),
                  reads=(b_rstd, b_sm), writes=(b_rstd,))
            pr.op(act, A(nc.scalar.activation, out=rstd_t[:], in_=rstd_t[:], func=AF.Exp, scale=-0.5),
                  reads=(b_rstd,), writes=(b_rstd,))

        def stage_norm(tt):
            tsl = slice(tt * 512, (tt + 1) * 512)
            j = tt % 2
            rr = X[j]
            for oc in range(8):
                pr.op(dve, A(nc.vector.tensor_tensor, out=sqq[:, oc, :], in0=rr[:, oc, :], in1=mean_t[:], op=ALU.subtract),
                      reads=(b_X[j][oc], b_mean), writes=(b_sqq[oc],))
                pr.op(dve, A(nc.vector.tensor_tensor, out=sqq[:, oc, :], in0=sqq[:, oc, :], in1=rstd_t[:], op=ALU.mult),
                      reads=(b_sqq[oc], b_rstd), writes=(b_sqq[oc],))
                pr.op(act, A(nc.scalar.activation, out=sqq[:, oc, :], in_=sqq[:, oc, :], func=AF.Identity,
                             scale=lng_t[:, oc:oc + 1], bias=lnb_t[:, oc:oc + 1]),
                      reads=(b_sqq[oc], b_const), writes=(b_sqq[oc],))
            pr.dma(sp, [(outT[s].rearrange("(c p) n -> p c n", p=128)[:, :, tsl], sqq[:, :, :])], s_out, reads=tuple(b_sqq))

        load_res(0)
        if NT > 1:
            load_res(1)
        load_p(0)
        phase_pb()
        chk("pb")
        stage_wo(0)
        for tt in range(NT):
            stage_gate(tt)
            if tt + 1 < NT:
                stage_wo(tt + 1)
            stage_stats(tt)
            stage_norm(tt)
            if s + 1 < NSEQ:
                nsl = max(1, NT - 1)
                per = (8 + nsl - 1) // nsl
                ks = range(tt * per, min(8, (tt + 1) * per))
                if len(ks):
                    load_x(s + 1, ks)
            if tt + 2 < NT:
                load_res(tt + 2)

    try:
        for s_ in range(NSEQ):
            run_seq(s_)
    except _Stop:
        pass

    pr._waits(sp, {s_out: s_out.count})

    with nc.Block() as block:
        @block.tensor
        def _(e):
            for f in pe.prog:
                f()

        @block.scalar
        def _(e):
            for f in act.prog:
                f()

        @block.vector
        def _(e):
            for f in dve.prog:
                f()

        @block.gpsimd
        def _(e):
            for f in pool.prog:
                f()

        @block.sync
        def _(e):
            for f in sp.prog:
                f()

    return nc, pr


def make_consts(S):
    pos = np.arange(S)
    ph = (pos // 128).astype(np.float32)
    pl = (pos % 128).astype(np.float32)
    one = np.ones(S, np.float32)
    qaug = np.stack([-128.0 * ph, -pl, one, one]).astype(np.float32)
    kaug = np.stack([one, one, 128.0 * ph, pl]).astype(np.float32)
    i = np.arange(128)
    dtab = (-2.0 * np.maximum(i[:, None] - i[None, :], 0)).astype(np.float32)
    ident = np.eye(128, dtype=np.float32)
    return {"c_qaug": qaug, "c_kaug": kaug, "c_kaugn": np.ascontiguousarray(-kaug), "c_dtab": dtab, "c_ident": ident}


def host_layout(inputs, S, NSEQ, n_cores):
    f = lambda a: np.ascontiguousarray(np.asarray(a, dtype=np.float32))
    x = np.asarray(inputs["x"], dtype=np.float32)
    p = np.asarray(inputs["p"], dtype=np.float32)[0]
    shared = {
        "w_in": f(inputs["w_in"][0]), "w_pa": f(inputs["w_proj_a"][0]), "w_pb": f(inputs["w_proj_b"][0]),
        "w_out": f(inputs["w_out"][0]), "w_pg": f(inputs["w_ple_gate"][0]), "w_ple": f(inputs["w_ple"][0]),
        "cw": f(np.asarray(inputs["conv_w"])[0].reshape(3, 8, 128).transpose(2, 1, 0)),
        "cb": f(np.asarray(inputs["conv_b"])[0].reshape(8, 128).T),
        "subg": f(np.asarray(inputs["subln_g"])[0].reshape(128, 1)),
        "lng": f(np.asarray(inputs["ln_g"])[0].reshape(8, 128).T),
        "lnb": f(np.asarray(inputs["ln_b"])[0].reshape(8, 128).T),
        "lamv": f(np.broadcast_to(np.stack([np.asarray(inputs[k])[0] for k in
                                            ("lambda_q1", "lambda_k1", "lambda_q2", "lambda_k2")])[None], (128, 4, 64))),
    }
    shared.update(make_consts(S))
    in_maps = []
    for c in range(n_cores):
        m = dict(shared)
        m["xT"] = np.ascontiguousarray(x[c * NSEQ:(c + 1) * NSEQ].transpose(0, 2, 1))
        m["pT"] = np.ascontiguousarray(p[c * NSEQ:(c + 1) * NSEQ].transpose(0, 2, 1))
        in_maps.append(m)
    return in_maps


_CACHE = {}


def kernel(**inputs):
    S, NSEQ = 2048, 4
    in_maps = host_layout(inputs, S, NSEQ, N_CORES)
    if "nc" not in _CACHE:
        _CACHE["nc"] = build(S, NSEQ)[0]
    res = run_bass_kernel_spmd(_CACHE["nc"], in_maps, core_ids=list(range(N_CORES)))
    outs = [r["outT"].transpose(0, 2, 1) for r in res.results]
    return np.ascontiguousarray(np.concatenate(outs, axis=0)).astype(np.float32)
```
